# Optimizing a Trainium2 kernel written in Bass

```python
import math
import jax, jax.numpy as jnp
from jax import lax
import numpy as np

D_MODEL = 1024
BATCH = 16
SEQ = 2048
DEPTH = 2

GRID_W = 64
LRU_WIDTH = D_MODEL
LRU_BLOCKS = 8
LRU_BW = LRU_WIDTH // LRU_BLOCKS
CONV_W = 4
CONV_PAD = (2, 1)
RG_C = 8.0
HEAD_DIM = 128
N_HEADS = D_MODEL // HEAD_DIM
N_KV_HEADS = 2
GROUP = N_HEADS // N_KV_HEADS
Q_BLOCK = 128
ROPE_THETA = 10000.0
AXIS_FREQS = HEAD_DIM // 4
D_FF = 4 * D_MODEL
EPS = 1e-6
N_RG = (DEPTH + 1) // 2
N_AT = DEPTH // 2

kernel_name = "hybrid_rglru_axial_gqa_encoder"


def rms_norm(x, g):
    xf = x.astype(jnp.float32)
    y = xf * lax.rsqrt(jnp.mean(xf * xf, axis=-1, keepdims=True) + EPS)
    return (y * g.astype(jnp.float32)).astype(x.dtype)


def rglru_direction(x, w_a, b_a, w_x, b_x, lam, reverse):
    B, L, C = x.shape
    xb = x.reshape(B, L, LRU_BLOCKS, LRU_BW)
    r = jax.nn.sigmoid((jnp.einsum('blhi,hij->blhj', xb, w_a).reshape(B, L, C) + b_a).astype(jnp.float32))
    i = jax.nn.sigmoid((jnp.einsum('blhi,hij->blhj', xb, w_x).reshape(B, L, C) + b_x).astype(jnp.float32))
    log_a = -RG_C * r * jax.nn.softplus(-lam.astype(jnp.float32))
    a = jnp.exp(log_a)
    mult = jnp.sqrt(-jnp.expm1(2.0 * log_a))
    u = mult * (i * x.astype(jnp.float32))

    def combine(p, q):
        a1, b1 = p
        a2, b2 = q
        return a1 * a2, a2 * b1 + b2

    _, h = lax.associative_scan(combine, (a, u), axis=1, reverse=reverse)
    return h.astype(x.dtype)


def rglru_block(h, w_in, conv_w, conv_b, w_a, b_a, w_x, b_x, lam, w_out):
    z = h @ w_in
    gate, rec = jnp.split(z, 2, axis=-1)
    gate = jax.nn.gelu(gate)
    rec = lax.conv_general_dilated(rec, conv_w, window_strides=(1,), padding=[CONV_PAD],
                                   dimension_numbers=('NWC', 'WIO', 'NWC'),
                                   feature_group_count=LRU_WIDTH) + conv_b
    y = (rglru_direction(rec, w_a[0], b_a[0], w_x[0], b_x[0], lam[0], False)
         + rglru_direction(rec, w_a[1], b_a[1], w_x[1], b_x[1], lam[1], True))
    return (y * gate) @ w_out


def axial_rope_tables(L):
    rows = L // GRID_W
    row = jnp.repeat(jnp.arange(rows, dtype=jnp.float32), GRID_W)
    col = jnp.tile(jnp.arange(GRID_W, dtype=jnp.float32), rows)
    inv = ROPE_THETA ** (-jnp.arange(AXIS_FREQS, dtype=jnp.float32) / AXIS_FREQS)
    ang_r = row[:, None] * inv
    ang_c = col[:, None] * inv
    return (jnp.cos(ang_r)[:, None, :], jnp.sin(ang_r)[:, None, :],
            jnp.cos(ang_c)[:, None, :], jnp.sin(ang_c)[:, None, :])


def rope_half(x, cos, sin):
    x1, x2 = jnp.split(x, 2, axis=-1)
    return jnp.concatenate([x1 * cos - x2 * sin, x2 * cos + x1 * sin], axis=-1)


def apply_axial_rope(x, tabs):
    cr, sr, cc, sc = tabs
    xf = x.astype(jnp.float32)
    xr, xc = jnp.split(xf, 2, axis=-1)
    return jnp.concatenate([rope_half(xr, cr, sr), rope_half(xc, cc, sc)], axis=-1).astype(x.dtype)


def attention_block(h, w_qkv, q_g, k_g, w_o):
    B, L, _ = h.shape
    qkv = h @ w_qkv
    q, k, v = jnp.split(qkv, [N_HEADS * HEAD_DIM, (N_HEADS + N_KV_HEADS) * HEAD_DIM], axis=-1)
    q = rms_norm(q.reshape(B, L, N_HEADS, HEAD_DIM), q_g)
    k = rms_norm(k.reshape(B, L, N_KV_HEADS, HEAD_DIM), k_g)
    v = v.reshape(B, L, N_KV_HEADS, HEAD_DIM)
    tabs = axial_rope_tables(L)
    q = apply_axial_rope(q, tabs)
    k = apply_axial_rope(k, tabs)
    nb = L // Q_BLOCK
    qb = q.reshape(B, nb, Q_BLOCK, N_KV_HEADS, GROUP, HEAD_DIM).transpose(1, 0, 2, 3, 4, 5)
    scale = 1.0 / math.sqrt(HEAD_DIM)

    def attend(qblk):
        s = jnp.einsum('bqkgd,bskd->bkgqs', qblk, k).astype(jnp.float32) * scale
        p = jax.nn.softmax(s, axis=-1).astype(v.dtype)
        return jnp.einsum('bkgqs,bskd->bqkgd', p, v)

    o = lax.map(attend, qb)
    o = o.transpose(1, 0, 2, 3, 4, 5).reshape(B, L, N_HEADS * HEAD_DIM)
    return o @ w_o


def sq_relu_mlp(h, w_up, w_down):
    u = jax.nn.relu(h @ w_up)
    return (u * u) @ w_down


def setup_inputs(seed: int = 0) -> dict:
    key = jax.random.key(seed)
    ks = jax.random.split(key, 24)
    f32 = jnp.float32
    nrm = lambda k, shape, fan_in: jax.random.normal(k, shape, f32) * (fan_in ** -0.5)
    gain = lambda k, shape: 1.0 + 0.02 * jax.random.normal(k, shape, f32)
    small = lambda k, shape: 0.01 * jax.random.normal(k, shape, f32)
    u = jax.random.uniform(ks[10], (N_RG, 2, LRU_WIDTH), f32, 0.9, 0.999)
    s = u ** (1.0 / RG_C)
    lam = jnp.log(s) - jnp.log1p(-s)
    return {
        "x": jax.random.normal(ks[0], (BATCH, SEQ, D_MODEL), f32),
        "norm_mix_g": gain(ks[1], (DEPTH, D_MODEL)),
        "norm_mlp_g": gain(ks[2], (DEPTH, D_MODEL)),
        "rg_w_in": nrm(ks[3], (N_RG, D_MODEL, 2 * LRU_WIDTH), D_MODEL),
        "rg_conv_w": nrm(ks[4], (N_RG, CONV_W, 1, LRU_WIDTH), CONV_W),
        "rg_conv_b": small(ks[5], (N_RG, LRU_WIDTH)),
        "rg_w_a": nrm(ks[6], (N_RG, 2, LRU_BLOCKS, LRU_BW, LRU_BW), LRU_BW),
        "rg_b_a": small(ks[7], (N_RG, 2, LRU_WIDTH)),
        "rg_w_x": nrm(ks[8], (N_RG, 2, LRU_BLOCKS, LRU_BW, LRU_BW), LRU_BW),
        "rg_b_x": small(ks[9], (N_RG, 2, LRU_WIDTH)),
        "rg_lam": lam,
        "rg_w_out": nrm(ks[11], (N_RG, LRU_WIDTH, D_MODEL), LRU_WIDTH),
        "at_w_qkv": nrm(ks[12], (N_AT, D_MODEL, (N_HEADS + 2 * N_KV_HEADS) * HEAD_DIM), D_MODEL),
        "at_q_g": gain(ks[13], (N_AT, HEAD_DIM)),
        "at_k_g": gain(ks[14], (N_AT, HEAD_DIM)),
        "at_w_o": nrm(ks[15], (N_AT, N_HEADS * HEAD_DIM, D_MODEL), N_HEADS * HEAD_DIM),
        "mlp_w_up": nrm(ks[16], (DEPTH, D_MODEL, D_FF), D_MODEL),
        "mlp_w_down": nrm(ks[17], (DEPTH, D_FF, D_MODEL), D_FF),
        "final_g": gain(ks[18], (D_MODEL,)),
    }


def reference(x, norm_mix_g, norm_mlp_g, rg_w_in, rg_conv_w, rg_conv_b, rg_w_a, rg_b_a,
              rg_w_x, rg_b_x, rg_lam, rg_w_out, at_w_qkv, at_q_g, at_k_g, at_w_o,
              mlp_w_up, mlp_w_down, final_g):
    for i in range(DEPTH):
        h = rms_norm(x, norm_mix_g[i])
        j = i // 2
        if i % 2 == 0:
            mix = rglru_block(h, rg_w_in[j], rg_conv_w[j], rg_conv_b[j], rg_w_a[j], rg_b_a[j],
                              rg_w_x[j], rg_b_x[j], rg_lam[j], rg_w_out[j])
        else:
            mix = attention_block(h, at_w_qkv[j], at_q_g[j], at_k_g[j], at_w_o[j])
        x = x + mix
        x = x + sq_relu_mlp(rms_norm(x, norm_mlp_g[i]), mlp_w_up[i], mlp_w_down[i])
    return rms_norm(x, final_g)
```

```python
import math
from contextlib import ExitStack

import numpy as np
import concourse.bass as bass
import concourse.mybir as mybir
from concourse.bass_utils import run_bass_kernel_spmd

F32 = mybir.dt.float32
BF16 = mybir.dt.bfloat16
AF = mybir.ActivationFunctionType
ALU = mybir.AluOpType

D = 1024
L = 2048
NCH = 8
TB = 512
NTB = L // TB
EPS = 1e-6
NSLOT_PER_SEQ = 47
FENCE = False
SYNC_SAME = True
SLOT_ELEMS = 4096


class Buf:
    __slots__ = ("name", "w", "r", "dsem", "dcnt")

    def __init__(self, name):
        self.name = name
        self.w = None
        self.r = {}
        self.dsem = None
        self.dcnt = 0


class Prog:
    ENGS = ("pe", "act", "dve", "pool", "sp")

    def __init__(self, nc, stack, sync_same=True):
        self.nc = nc
        self.stack = stack
        self.sync_same = sync_same
        self.q = {e: [] for e in self.ENGS}
        self.sem = {e: stack.enter_context(nc.semaphore("s_" + e)) for e in self.ENGS}
        self.cnt = {e: 0 for e in self.ENGS}
        self.seen = {e: {} for e in self.ENGS}
        self.nsem = 0
        self.dma_toks = []

    def _waits(self, engine, deps, skip_sem=None):
        out = []
        seen = self.seen[engine]
        for d in deps:
            if d is None:
                continue
            kind, key, n = d
            if kind == "eng":
                k = ("e", key)
                sem = self.sem[key]
            else:
                if skip_sem is not None and key is skip_sem:
                    continue
                k = ("d", key.num)
                sem = key
            if seen.get(k, 0) >= n:
                continue
            seen[k] = n
            out.append((sem, n))
        return out

    @staticmethod
    def _deps(reads, writes):
        deps = []
        for b in reads:
            if b.w is not None:
                deps.append(b.w)
        for b in writes:
            if b.w is not None:
                deps.append(b.w)
            deps.extend(b.r.values())
        return deps

    def op(self, engine, emit, reads=(), writes=(), same=None):
        deps = self._deps(reads, writes)
        if same is None:
            same = self.sync_same
        if not same:
            deps = [d for d in deps if not (d[0] == "eng" and d[1] == engine)]
        waits = self._waits(engine, deps)
        self.cnt[engine] += 1
        tok = ("eng", engine, self.cnt[engine])
        sem = self.sem[engine]

        def run(eng, waits=waits, emit=emit, sem=sem):
            for s, v in waits:
                eng.wait_ge(s, v)
            emit(eng).then_inc(sem, 1)

        self.q[engine].append(run)
        for b in reads:
            b.r[("e", engine)] = tok
        for b in writes:
            b.w = tok
            b.r = {}
        return tok

    def dma(self, queue, out_ap, in_ap, reads=(), writes=(), owner=None, **kw):
        if owner is None:
            owner = writes[0] if writes else reads[0]
        if owner.dsem is None:
            owner.dsem = self.stack.enter_context(self.nc.semaphore("d%d" % self.nsem))
            self.nsem += 1
        deps = self._deps(reads, writes)
        waits = self._waits(queue, deps, skip_sem=owner.dsem)
        owner.dcnt += 16
        tok = ("dma", owner.dsem, owner.dcnt)
        sem = owner.dsem

        def run(eng, waits=waits, sem=sem, out_ap=out_ap, in_ap=in_ap, kw=kw):
            for s, v in waits:
                eng.wait_ge(s, v)
            eng.dma_start(out=out_ap, in_=in_ap, **kw).then_inc(sem, 16)

        self.q[queue].append(run)
        for b in reads:
            b.r[("d", sem.num)] = tok
        for b in writes:
            b.w = tok
            b.r = {}
        self.dma_toks.append(tok)
        return tok

    def emit_all(self):
        nc = self.nc
        last = {}
        for t in self.dma_toks:
            last[t[1].num] = t
        toks = list(last.values()) + [("eng", e, self.cnt[e]) for e in self.ENGS
                                      if self.cnt[e] > 0 and e != "sp"]
        waits = self._waits("sp", toks)

        def fin(eng, waits=waits):
            for s, v in waits:
                eng.wait_ge(s, v)

        self.q["sp"].append(fin)
        q = self.q
        with nc.Block() as block:
            @block.tensor
            def _(eng):
                for f in q["pe"]:
                    f(eng)

            @block.scalar
            def _(eng):
                for f in q["act"]:
                    f(eng)

            @block.vector
            def _(eng):
                for f in q["dve"]:
                    f(eng)

            @block.gpsimd
            def _(eng):
                for f in q["pool"]:
                    f(eng)

            @block.sync
            def _(eng):
                for f in q["sp"]:
                    f(eng)


C_MIXG = 0
C_MLPG = 16
C_FING = 32
C_CONVW = 40
C_CONVB = 72
C_BA = 80
C_BX = 96
C_LAM = 112
C_QG = 128
C_KG = 129
NCONST = 130


def build_program(nseq=2, stop_after=4):
    nc = bass.Bass("TRN2", target_bir_lowering=False)
    xT_d = nc.dram_tensor("xT", [nseq, D, L], F32, kind="ExternalInput").ap()
    wst_d = nc.dram_tensor("wst", [NSLOT_PER_SEQ, 128, SLOT_ELEMS], F32, kind="ExternalInput").ap()
    cst_d = nc.dram_tensor("cst", [128, NCONST], F32, kind="ExternalInput").ap()
    rope_d = nc.dram_tensor("rope", [2, 128, L], F32, kind="ExternalInput").ap()
    mats_d = nc.dram_tensor("mats", [128, 256], F32, kind="ExternalInput").ap()
    out_d = nc.dram_tensor("outT", [nseq, D, L], F32, kind="ExternalOutput").ap()

    with ExitStack() as st:
        P = Prog(nc, st, sync_same=SYNC_SAME)
        sb = lambda n, s, d: st.enter_context(nc.sbuf_tensor(n, s, d))

        xt = sb("xt", [128, NCH * L], F32)
        ht = sb("ht", [128, NCH * L], BF16)
        yt = sb("yt", [128, NCH * L], BF16)
        Bx = [[Buf("x%d_%d" % (c, t)) for t in range(NTB)] for c in range(NCH)]
        Bh = [[Buf("h%d_%d" % (c, t)) for t in range(NTB)] for c in range(NCH)]
        By = [[Buf("y%d_%d" % (c, t)) for t in range(NTB)] for c in range(NCH)]

        def xv(c, t):
            return xt[:, c * L + t * TB: c * L + (t + 1) * TB]

        def hv(c, t):
            return ht[:, c * L + t * TB: c * L + (t + 1) * TB]

        def yv(c, t):
            return yt[:, c * L + t * TB: c * L + (t + 1) * TB]

        NRING = 4
        ring = [sb("ring%d" % i, [128, SLOT_ELEMS], BF16) for i in range(NRING)]
        Bring = [Buf("ring%d" % i) for i in range(NRING)]
        ring_n = [0]

        T8 = [sb("T8_%d" % i, [128, L], F32) for i in range(3)]
        BT8 = [[Buf("T8_%d_%d" % (i, j)) for j in range(8 if i < 2 else 4)] for i in range(3)]
        T4 = [sb("T4_%d" % i, [128, L], BF16) for i in range(2)]
        BT4 = [[Buf("T4_%d_%d" % (i, j)) for j in range(4)] for i in range(2)]
        NS = 6
        S = [sb("S%d" % i, [128, TB], F32) for i in range(NS)]
        BS = [Buf("S%d" % i) for i in range(NS)]

        cst = sb("cst_sb", [128, NCONST], F32)
        Bcst = Buf("cst")
        der = sb("der", [128, 64], F32)
        Bder = Buf("der")
        mats = sb("mats_sb", [128, 256], BF16)
        Bmats = Buf("mats")

        ps = st.enter_context(nc.psum_tensor("ps", [128, 8 * 512], F32))
        Bps = [Buf("ps%d" % i) for i in range(8)]

        def bank(i):
            return ps[:, i * 512:(i + 1) * 512]

        poolA = [0]
        poolB = [0]

        def nextA():
            i = poolA[0] % 4
            poolA[0] += 1
            return i

        def nextB():
            i = 4 + poolB[0] % 4
            poolB[0] += 1
            return i

        P.dma("sp", cst[:], cst_d[:], writes=[Bcst])
        P.dma("pool", mats[:], mats_d[:], writes=[Bmats])
        ones = mats[:, 0:128]
        rotT = mats[:, 128:256]
        P.op("act", lambda e: e.activation(der[:, 0:16], cst[:, C_LAM:C_LAM + 16], AF.Exp, scale=-1.0),
             reads=[Bcst], writes=[Bder])
        P.op("act", lambda e: e.activation(der[:, 0:16], der[:, 0:16], AF.Ln, bias=1.0, scale=1.0),
             reads=[Bder], writes=[Bder])
        P.op("dve", lambda e: e.tensor_scalar(der[:, 0:16], der[:, 0:16], -4.0, None, ALU.mult),
             reads=[Bder], writes=[Bder])
        P.op("dve", lambda e: e.tensor_scalar(der[:, 16:48], cst[:, C_BA:C_BA + 32], 0.5, None, ALU.mult),
             reads=[Bcst, Bder], writes=[Bder])

        def col(i):
            return cst[:, i:i + 1]

        def dcol(i):
            return der[:, i:i + 1]

        fence = Buf("fence")
        Bdbg = Buf("dbg")

        def wload(idx, nelem=SLOT_ELEMS):
            j = ring_n[0] % NRING
            ring_n[0] += 1
            P.dma("pool", ring[j][:, 0:nelem], wst_d[idx, :, 0:nelem], reads=[fence], writes=[Bring[j]], owner=Bring[j],
                  max_dma_last_dim=8192)
            return ring[j], Bring[j]

        sq_n = [0]

        def rmsnorm_tb(t, gbase, out_fn=None):
            pb = nextB()
            for c in range(NCH):
                k = sq_n[0] % 4
                sq_n[0] += 1
                sqv = T8[2][:].bitcast(BF16)[:, k * 1024:k * 1024 + TB]
                P.op("act", lambda e, c=c, sqv=sqv: e.activation(sqv, xv(c, t), AF.Square),
                     reads=[Bx[c][t]], writes=[BT8[2][k]])
                P.op("pe", lambda e, c=c, sqv=sqv, pb=pb: e.matmul(bank(pb), ones, sqv, start=(c == 0), stop=(c == NCH - 1)),
                     reads=[BT8[2][k], Bmats], writes=[Bps[pb]], same=False)
            si = 4 + (t % 2)
            P.op("act", lambda e, pb=pb, si=si: e.activation(S[si][:], bank(pb), AF.Ln, bias=dcol(48), scale=1.0 / D),
                 reads=[Bps[pb], Bder], writes=[BS[si]])
            P.op("act", lambda e, si=si: e.activation(S[si][:], S[si][:], AF.Exp, scale=-0.5),
                 reads=[BS[si]], writes=[BS[si]])
            for c in range(NCH):
                if out_fn is None:
                    P.op("dve", lambda e, c=c, si=si: e.scalar_tensor_tensor(hv(c, t), xv(c, t), col(gbase + c), S[si][:], ALU.mult, ALU.mult),
                         reads=[Bx[c][t], BS[si], Bcst], writes=[Bh[c][t]])
                else:
                    out_fn(c, t, si)

        P.op("dve", lambda e: e.memset(der[:, 48:49], EPS), reads=[], writes=[Bder])

        for s in range(nseq):
            wbase = 0
            if FENCE and s > 0:
                fence.w = ("eng", "dve", P.cnt["dve"])
            def load_x(s_, t):
                for c in range(NCH):
                    P.dma("sp", xv(c, t), xT_d[s_, c * 128:(c + 1) * 128, t * TB:(t + 1) * TB], writes=[Bx[c][t]])
            if s == 0 or stop_after < 4:
                for t in range(NTB):
                    load_x(s, t)

            for t in range(NTB):
                if s > 0 and stop_after >= 4 and t < NTB - 1:
                    continue
                rmsnorm_tb(t, C_MIXG + 0)

            T4f = [T4[0][:].bitcast(F32), T4[1][:].bitcast(F32)]
            UT = [(S[k][:], [BS[k]]) for k in range(NS)]
            for i in range(2):
                for m in range(2):
                    UT.append((T4f[i][:, m * TB:(m + 1) * TB], [BT4[i][2 * m], BT4[i][2 * m + 1]]))
            for m in range(4):
                UT.append((T8[2][:, m * TB:(m + 1) * TB], [BT8[2][m]]))
            ut_n = [0, 0, 0]
            UTm, UTau = UT[0:2], UT[2:10]
            UTacc = [(T8[2][:, 0:2 * TB], [BT8[2][0], BT8[2][1]]), (T8[2][:, 2 * TB:4 * TB], [BT8[2][2], BT8[2][3]])]

            def ut_next(kind=2):
                pool_ = (UTm, UTacc, UTau)[kind]
                r = pool_[ut_n[kind] % len(pool_)]
                ut_n[kind] += 1
                return r

            rb = [T8[0][:].bitcast(BF16), T8[1][:].bitcast(BF16)]
            Rg = ps[:, 0:L]
            wts = {}

            def rec_mm(c):
                wt, Bw = wload(wbase + c, 2560)
                wts[c] = (wt, Bw)
                for t in range(NTB):
                    def mm_rec(e, t=t, wt=wt):
                        ins = None
                        for k in range(NCH):
                            ins = e.matmul(bank(t), wt[:, k * 256 + 128:k * 256 + 256], hv(k, t), start=(k == 0), stop=(k == NCH - 1))
                        return ins
                    P.op("pe", mm_rec, reads=[Bw] + [Bh[k][t] for k in range(NCH)], writes=[Bps[t]], same=False)

            gate_pb = {}

            def gate_mm(c):
                wt, Bw = wts[c]
                for t in range(NTB):
                    pb = nextB()
                    gate_pb[(c, t)] = pb

                    def mm_gate(e, t=t, wt=wt, pb=pb):
                        ins = None
                        for k in range(NCH):
                            ins = e.matmul(bank(pb), wt[:, k * 256:k * 256 + 128], hv(k, t), start=(k == 0), stop=(k == NCH - 1))
                        return ins
                    P.op("pe", mm_gate, reads=[Bw] + [Bh[k][t] for k in range(NCH)], writes=[Bps[pb]], same=False)

            def gelu(c):
                for t in range(NTB):
                    pb = gate_pb.pop((c, t))
                    P.op("act", lambda e, t=t, pb=pb, c=c: e.activation(yv(c, t), bank(pb), AF.Gelu_apprx_tanh),
                         reads=[Bps[pb]], writes=[By[c][t]])

            conv_acc = {}

            def conv(c):
                W2 = 2 * TB
                for t2 in range(2):
                    acc, Bacc = ut_next(1)
                    conv_acc[(c, t2)] = (acc, Bacc)
                    base = t2 * W2
                    rps = [Bps[b] for b in ((0, 1, 2) if t2 == 0 else (1, 2, 3))]
                    P.op("act", lambda e, c=c, acc=acc, base=base: e.activation(acc, Rg[:, base:base + W2], AF.Identity, bias=col(C_CONVB + c), scale=col(C_CONVW + 2 * 8 + c)),
                         reads=rps + [Bcst], writes=Bacc)
                    lo = 1 if t2 == 0 else 0
                    P.op("dve", lambda e, c=c, acc=acc, base=base, lo=lo: e.scalar_tensor_tensor(acc[:, lo:W2], Rg[:, base + lo - 1:base + W2 - 1], col(C_CONVW + 1 * 8 + c), acc[:, lo:W2], ALU.mult, ALU.add),
                         reads=rps + [Bcst] + Bacc, writes=Bacc)
                    lo = 2 if t2 == 0 else 0
                    P.op("dve", lambda e, c=c, acc=acc, base=base, lo=lo: e.scalar_tensor_tensor(acc[:, lo:W2], Rg[:, base + lo - 2:base + W2 - 2], col(C_CONVW + 0 * 8 + c), acc[:, lo:W2], ALU.mult, ALU.add),
                         reads=rps + [Bcst] + Bacc, writes=Bacc)
                    hi = W2 - 1 if t2 == 1 else W2
                    i2 = c % 2
                    Brb = [BT8[i2][2 * t2], BT8[i2][2 * t2 + 1]]
                    P.op("dve", lambda e, c=c, acc=acc, base=base, hi=hi, i2=i2: e.scalar_tensor_tensor(rb[i2][:, base:base + hi], Rg[:, base + 1:base + 1 + hi], col(C_CONVW + 3 * 8 + c), acc[:, 0:hi], ALU.mult, ALU.add),
                         reads=rps + [Bcst] + Bacc, writes=Brb)
                    if t2 == 1:
                        P.op("dve", lambda e, acc=acc, i2=i2: e.tensor_copy(rb[i2][:, L - 1:L], acc[:, W2 - 1:W2]),
                             reads=Bacc + Brb, writes=Brb)

            def conv_cast(c):
                W2 = 2 * TB
                i2 = c % 2
                for t2 in range(2):
                    conv_acc.pop((c, t2))

            ustate = {}

            def unit_A(c, d, t):
                i2 = c % 2
                wt, Bw = wts[c]
                recv = rb[i2][:, t * TB:(t + 1) * TB]
                Brec = BT8[i2][t]
                (A, BA), (U, BU) = ut_next(), ut_next()
                pa = nextB()
                P.op("pe", lambda e, d=d, pa=pa, wt=wt, recv=recv: e.matmul(bank(pa), wt[:, 2048 + (d * 2 + 0) * 128:2048 + (d * 2 + 1) * 128], recv, start=True, stop=True),
                     reads=[Bw, Brec], writes=[Bps[pa]], same=False)
                P.op("act", lambda e, d=d, c=c, pa=pa, A=A: e.activation(A, bank(pa), AF.Tanh, bias=dcol(16 + d * 8 + c), scale=0.5),
                     reads=[Bps[pa], Bder], writes=BA)
                px = nextB()
                P.op("pe", lambda e, d=d, px=px, wt=wt, recv=recv: e.matmul(bank(px), wt[:, 2048 + (d * 2 + 1) * 128:2048 + (d * 2 + 2) * 128], recv, start=True, stop=True),
                     reads=[Bw, Brec], writes=[Bps[px]], same=False)
                P.op("act", lambda e, d=d, c=c, px=px, U=U: e.activation(U, bank(px), AF.Tanh, bias=dcol(32 + d * 8 + c), scale=0.5),
                     reads=[Bps[px], Bder], writes=BU)
                P.op("act", lambda e, d=d, c=c, A=A: e.activation(A, A, AF.Exp, bias=dcol(d * 8 + c), scale=dcol(d * 8 + c)),
                     reads=BA + [Bder], writes=BA)
                ustate[(c, d, t)] = (A, BA, U, BU, recv, Brec)

            def unit_B(c, d, t, prev):
                i2 = c % 2
                A, BA, U, BU, recv, Brec = ustate[(c, d, t)]
                M, BM = ut_next(0)
                P.op("act", lambda e, A=A, M=M: e.activation(M, A, AF.Square),
                     reads=BA, writes=BM)
                P.op("act", lambda e, M=M: e.activation(M, M, AF.Sqrt, bias=1.0, scale=-1.0),
                     reads=BM, writes=BM)
                P.op("dve", lambda e, U=U, M=M: e.scalar_tensor_tensor(U, U, 1.0, M, ALU.add, ALU.mult),
                     reads=BU + BM, writes=BU)
                P.op("dve", lambda e, U=U, recv=recv: e.scalar_tensor_tensor(U, U, 0.5, recv, ALU.mult, ALU.mult),
                     reads=BU + [Brec], writes=BU)
                stv = rb[i2][:, L + t * TB:L + (t + 1) * TB]
                Bst = BT8[i2][4 + t]
                if d == 0:
                    if prev is None:
                        init, Bin = 0.0, []
                    else:
                        tp_ = prev[2]
                        init, Bin = rb[i2][:, L + tp_ * TB + TB - 1:L + tp_ * TB + TB], [BT8[i2][4 + tp_]]
                    P.op("dve", lambda e, U=U, A=A, init=init, stv=stv: e.tensor_tensor_scan(stv, A, U, init, ALU.mult, ALU.add),
                         reads=BA + BU + Bin, writes=[Bst])
                else:
                    if prev is None:
                        init, Bin = 0.0, []
                    else:
                        init, Bin = ustate[prev][2][:, 0:1], ustate[prev][3]
                    P.op("dve", lambda e, U=U, A=A, init=init: e.tensor_tensor_scan(U[:, ::-1], A[:, ::-1], U[:, ::-1], init, ALU.mult, ALU.add),
                         reads=BA + BU + Bin, writes=BU)
                    P.op("dve", lambda e, U=U, A=A, stv=stv: e.tensor_tensor(A, U, stv, ALU.add),
                         reads=BU + [Bst], writes=BA)
                    P.op("dve", lambda e, A=A, c=c, t=t: e.tensor_tensor(yv(c, t), A, yv(c, t), ALU.mult),
                         reads=BA + [By[c][t]], writes=[By[c][t]])
                if prev is not None:
                    del ustate[prev]

            rec_mm(0)
            conv(0)
            conv_cast(0)
            gate_mm(0)
            gelu(0)
            for c in range(NCH):
                order0 = [0, 1, 2, 3]
                order1 = [3, 2, 1, 0]
                for t in order0:
                    unit_A(c, 0, t)
                if c + 1 < NCH:
                    rec_mm(c + 1)
                prev = None
                for t in order0:
                    unit_B(c, 0, t, prev)
                    prev = (c, 0, t)
                last0 = prev
                if c + 1 < NCH:
                    conv(c + 1)
                for t in order1:
                    unit_A(c, 1, t)
                if c + 1 < NCH:
                    conv_cast(c + 1)
                    gate_mm(c + 1)
                prev = None
                for t in order1:
                    unit_B(c, 1, t, prev)
                    prev = (c, 1, t)
                del ustate[last0]
                del ustate[prev]
                if c + 1 < NCH:
                    gelu(c + 1)
            wbase += 8

            def out_proj(widx, after_tb=None):
                for s2 in range(2):
                    wt, Bw = wload(widx + s2)
                    for t in range(NTB):
                        for dd in range(4):
                            pb = nextB()

                            def mm(e, wt=wt, t=t, dd=dd, pb=pb):
                                ins = None
                                for c in range(NCH):
                                    ins = e.matmul(bank(pb), wt[:, c * 512 + dd * 128:c * 512 + (dd + 1) * 128], yv(c, t), start=(c == 0), stop=(c == NCH - 1))
                                return ins
                            P.op("pe", mm, reads=[Bw] + [By[c][t] for c in range(NCH)], writes=[Bps[pb]], same=False)
                            dc = s2 * 4 + dd
                            P.op("dve", lambda e, dc=dc, t=t, pb=pb: e.tensor_tensor(xv(dc, t), xv(dc, t), bank(pb), ALU.add),
                                 reads=[Bx[dc][t], Bps[pb]], writes=[Bx[dc][t]])
                        if s2 == 1 and after_tb is not None:
                            after_tb(t)

            def mlp(widx, gbase, do_norm=True, after_tb=None):
                if do_norm:
                    for t in range(NTB):
                        rmsnorm_tb(t, gbase)
                items = [(g, t) for g in range(8) for t in range(NTB)]
                slots = {}
                actbuf = [T4[0], T4[1]]
                Bact = [BT4[0], BT4[1]]

                def up(i):
                    g, t = items[i]
                    if t == 0:
                        slots[g] = (wload(widx + 2 * g), wload(widx + 2 * g + 1))
                    (wu, Bwu), _ = slots[g]
                    ab = i % 2
                    for f in range(4):
                        pa = nextA()

                        def mm(e, wu=wu, f=f, t=t, pa=pa):
                            ins = None
                            for k in range(NCH):
                                ins = e.matmul(bank(pa), wu[:, k * 512 + f * 128:k * 512 + (f + 1) * 128], hv(k, t), start=(k == 0), stop=(k == NCH - 1))
                            return ins
                        P.op("pe", mm, reads=[Bwu] + [Bh[k][t] for k in range(NCH)], writes=[Bps[pa]], same=False)
                        si = f % 4
                        P.op("act", lambda e, pa=pa, si=si: e.activation(S[si][:], bank(pa), AF.Relu),
                             reads=[Bps[pa]], writes=[BS[si]])
                        P.op("dve", lambda e, si=si, ab=ab, f=f: e.tensor_tensor(actbuf[ab][:, f * TB:(f + 1) * TB], S[si][:], S[si][:], ALU.mult),
                             reads=[BS[si]], writes=[Bact[ab][f]])

                def down(i):
                    g, t = items[i]
                    _, (wd, Bwd) = slots[g]
                    ab = i % 2
                    for dc in range(NCH):
                        pb = nextB()

                        def mm(e, wd=wd, dc=dc, pb=pb, ab=ab):
                            ins = None
                            for f in range(4):
                                ins = e.matmul(bank(pb), wd[:, f * 1024 + dc * 128:f * 1024 + (dc + 1) * 128], actbuf[ab][:, f * TB:(f + 1) * TB], start=(f == 0), stop=(f == 3))
                            return ins
                        P.op("pe", mm, reads=[Bwd] + Bact[ab], writes=[Bps[pb]], same=False)
                        P.op("dve", lambda e, dc=dc, t=t, pb=pb: e.tensor_tensor(xv(dc, t), xv(dc, t), bank(pb), ALU.add),
                             reads=[Bx[dc][t], Bps[pb]], writes=[Bx[dc][t]])
                    if g == 7 and after_tb is not None:
                        after_tb(t)

                for i in range(len(items) + 1):
                    if i < len(items):
                        up(i)
                    if i >= 1:
                        down(i - 1)

            if stop_after >= 2:
                out_proj(wbase, after_tb=lambda t: rmsnorm_tb(t, C_MLPG + 0))
            else:
                out_proj(wbase)
            wbase += 2
            if stop_after >= 3:
                mlp(wbase, C_MLPG + 0, do_norm=False, after_tb=lambda t: rmsnorm_tb(t, C_MIXG + 8))
            elif stop_after >= 2:
                mlp(wbase, C_MLPG + 0, do_norm=False)
            wbase += 16

            if stop_after >= 3:
                wkv, Bwkv = wload(wbase + 0)
                wq = [wload(wbase + 1), wload(wbase + 2)]
                kT = T8[0][:].bitcast(BF16)
                BkT = BT8[0]
                Vt = T8[1][:].bitcast(BF16)
                BV = BT8[1]
                cs_t = [(T8[2][:, 0:TB], T8[2][:, TB:2 * TB], BT8[2][0], BT8[2][1]),
                        (T8[2][:, 2 * TB:3 * TB], T8[2][:, 3 * TB:4 * TB], BT8[2][2], BT8[2][3])]
                pitems = []
                for t in range(NTB):
                    for j in range(10):
                        pitems.append((t, j))
                    for tt in range(4):
                        pitems.append((t, 10 + tt))
                pstate = {}

                def p1(i):
                    t, j = pitems[i]
                    if j == 0:
                        cosv, sinv, Bcos, Bsin = cs_t[t % 2]
                        P.dma("sp", cosv, rope_d[0, :, t * TB:(t + 1) * TB], writes=[Bcos])
                        P.dma("sp", sinv, rope_d[1, :, t * TB:(t + 1) * TB], writes=[Bsin])
                    if j >= 10:
                        tile_i = t * 4 + (j - 10)
                        pa = nextA()

                        def mmv(e, pa=pa, tile_i=tile_i, wkv=wkv):
                            ins = None
                            for k in range(NCH):
                                ins = e.matmul(ps[:, pa * 512:pa * 512 + 256], ht[:, k * L + tile_i * 128:k * L + (tile_i + 1) * 128], wkv[:, k * 512 + 256:k * 512 + 512], start=(k == 0), stop=(k == NCH - 1))
                            return ins
                        P.op("pe", mmv, reads=[Bwkv] + [Bh[k][t] for k in range(NCH)], writes=[Bps[pa]], same=False)
                        P.op("act", lambda e, pa=pa, tile_i=tile_i: e.activation(Vt[:, tile_i * 256:(tile_i + 1) * 256], ps[:, pa * 512:pa * 512 + 256], AF.Copy),
                             reads=[Bps[pa]], writes=[BV[tile_i // 2]])
                        return
                    if j < 2:
                        wt, Bw, cofs, gcolm = wkv, Bwkv, j * 128, C_KG
                        dest = kT[:, j * L + t * TB: j * L + (t + 1) * TB]
                        Bdest = BkT[(j * L + t * TB) // 512]
                    else:
                        hq = j - 2
                        (wt, Bw) = wq[hq // 4]
                        cofs, gcolm = (hq % 4) * 128, C_QG
                        dest = yv(hq, t)
                        Bdest = By[hq][t]
                    pa = nextA()

                    def mm(e, wt=wt, cofs=cofs, t=t, pa=pa):
                        ins = None
                        for k in range(NCH):
                            ins = e.matmul(bank(pa), wt[:, k * 512 + cofs:k * 512 + cofs + 128], hv(k, t), start=(k == 0), stop=(k == NCH - 1))
                        return ins
                    P.op("pe", mm, reads=[Bw] + [Bh[k][t] for k in range(NCH)], writes=[Bps[pa]], same=False)
                    kk = qn_n[0] % 4
                    qn_n[0] += 1
                    sqv = T4[0][:, kk * TB:(kk + 1) * TB]
                    P.op("act", lambda e, sqv=sqv, pa=pa: e.activation(sqv, bank(pa), AF.Square),
                         reads=[Bps[pa]], writes=[BT4[0][kk]])
                    pstate[i] = dict(pa=pa, kk=kk, sqv=sqv, gcolm=gcolm, dest=dest, Bdest=Bdest)

                def p2(i):
                    if i not in pstate:
                        return
                    d = pstate[i]
                    pa, kk, sqv, gcolm = d["pa"], d["kk"], d["sqv"], d["gcolm"]
                    qnv = T4[1][:, kk * TB:(kk + 1) * TB]
                    pb = nextB()
                    P.op("pe", lambda e, pb=pb, sqv=sqv: e.matmul(bank(pb), ones, sqv, start=True, stop=True),
                         reads=[BT4[0][kk], Bmats], writes=[Bps[pb]], same=False)
                    si = kk % 2
                    P.op("act", lambda e, pb=pb, si=si: e.activation(S[si][:], bank(pb), AF.Ln, bias=dcol(48), scale=1.0 / 128),
                         reads=[Bps[pb], Bder], writes=[BS[si]])
                    P.op("act", lambda e, si=si: e.activation(S[si][:], S[si][:], AF.Exp, scale=-0.5),
                         reads=[BS[si]], writes=[BS[si]])
                    P.op("dve", lambda e, qnv=qnv, pa=pa, si=si, gcolm=gcolm: e.scalar_tensor_tensor(qnv, bank(pa), col(gcolm), S[si][:], ALU.mult, ALU.mult),
                         reads=[Bps[pa], BS[si], Bcst], writes=[BT4[1][kk]])
                    d["qnv"] = qnv

                def p3(i):
                    if i not in pstate:
                        return
                    d = pstate.pop(i)
                    t, j = pitems[i]
                    cosv, sinv, Bcos, Bsin = cs_t[t % 2]
                    kk, qnv, dest, Bdest = d["kk"], d["qnv"], d["dest"], d["Bdest"]
                    pr = nextB()
                    P.op("pe", lambda e, pr=pr, qnv=qnv: e.matmul(bank(pr), rotT, qnv, start=True, stop=True),
                         reads=[BT4[1][kk], Bmats], writes=[Bps[pr]], same=False)
                    s1, s2 = 2 + 2 * (kk % 2), 3 + 2 * (kk % 2)
                    P.op("dve", lambda e, qnv=qnv, s1=s1, cosv=cosv: e.tensor_tensor(S[s1][:], qnv, cosv, ALU.mult),
                         reads=[BT4[1][kk], Bcos], writes=[BS[s1]])
                    P.op("dve", lambda e, pr=pr, s2=s2, sinv=sinv: e.tensor_tensor(S[s2][:], bank(pr), sinv, ALU.mult),
                         reads=[Bps[pr], Bsin], writes=[BS[s2]])
                    P.op("dve", lambda e, dest=dest, s1=s1, s2=s2: e.tensor_tensor(dest, S[s1][:], S[s2][:], ALU.add),
                         reads=[BS[s1], BS[s2]], writes=[Bdest])

                qn_n = [0]
                NP = len(pitems)
                for i in range(NP + 3):
                    if i < NP:
                        p1(i)
                    if 1 <= i < NP + 1:
                        p2(i - 1)
                    if i >= 3:
                        p3(i - 3)
                wbase_att = wbase + 3

                scale = 1.0 / math.sqrt(128.0)
                items = [(h, qb, kt) for h in range(8) for qb in range(NTB) for kt in range(16)]
                pt_n = [0]
                state = {}

                def s_part(i):
                    h, qb, kt = items[i]
                    kv = h // 4
                    pa = nextA()
                    kcol = kv * L + kt * 128
                    P.op("pe", lambda e, pa=pa, kcol=kcol, h=h, qb=qb: e.matmul(bank(pa), kT[:, kcol:kcol + 128], yv(h, qb), start=True, stop=True),
                         reads=[BkT[kcol // 512], By[h][qb]], writes=[Bps[pa]], same=False)
                    kk = pt_n[0] % 4
                    pt_n[0] += 1
                    ptv = T4[0][:, kk * TB:(kk + 1) * TB]
                    P.op("act", lambda e, pa=pa, ptv=ptv: e.activation(ptv, bank(pa), AF.Exp, scale=scale),
                         reads=[Bps[pa]], writes=[BT4[0][kk]])
                    state[i] = (kk, ptv)

                def pv_part(i):
                    h, qb, kt = items[i]
                    kv = h // 4
                    kk, ptv = state.pop(i)
                    n = (h * NTB + qb) % 2
                    po, psm = 4 + n, 6 + n
                    vcol = kt * 256 + kv * 128

                    def mm(e, po=po, psm=psm, vcol=vcol, ptv=ptv, kt=kt):
                        e.matmul(bank(po), Vt[:, vcol:vcol + 128], ptv, start=(kt == 0), stop=(kt == 15))
                        return e.matmul(bank(psm), ones, ptv, start=(kt == 0), stop=(kt == 15))
                    P.op("pe", mm, reads=[BV[kt // 2], BT4[0][kk], Bmats], writes=[Bps[po], Bps[psm]], same=False)
                    if kt == 15:
                        si = n
                        P.op("dve", lambda e, si=si, psm=psm: e.reciprocal(S[si][:], bank(psm)),
                             reads=[Bps[psm]], writes=[BS[si]])
                        P.op("dve", lambda e, si=si, po=po, h=h, qb=qb: e.tensor_tensor(yv(h, qb), bank(po), S[si][:], ALU.mult),
                             reads=[Bps[po], BS[si]], writes=[By[h][qb]])

                SK = 2
                for i in range(len(items) + SK):
                    if i < len(items):
                        s_part(i)
                    if i >= SK:
                        pv_part(i - SK)

                if stop_after >= 4:
                    out_proj(wbase_att, after_tb=lambda t: rmsnorm_tb(t, C_MLPG + 8))
                else:
                    out_proj(wbase_att)
            wbase += 5

            if stop_after >= 4:
                on = [0]

                def fin_out(c, t, si, s=s):
                    k = on[0] % 4
                    on[0] += 1
                    P.op("dve", lambda e, c=c, t=t, si=si, k=k: e.scalar_tensor_tensor(S[k][:], xv(c, t), col(C_FING + c), S[si][:], ALU.mult, ALU.mult),
                         reads=[Bx[c][t], BS[si], Bcst], writes=[BS[k]])
                    P.dma("sp", out_d[s, c * 128:(c + 1) * 128, t * TB:(t + 1) * TB], S[k][:], reads=[BS[k]])
                def fin_cb(t, s=s):
                    rmsnorm_tb(t, C_FING, out_fn=fin_out)
                    if s + 1 < nseq:
                        load_x(s + 1, t)
                        if t >= 1:
                            rmsnorm_tb(t - 1, C_MIXG + 0)
                mlp(wbase, C_MLPG + 8, do_norm=False, after_tb=fin_cb)
            wbase += 16
            if stop_after >= 4:
                pass
            else:
                for c in range(NCH):
                    P.dma("sp", out_d[s, c * 128:(c + 1) * 128, :], xt[:, c * L:(c + 1) * L], reads=Bx[c], owner=Bdbg)

        P.emit_all()
    return nc


def prep_weights(inp):
    f = np.float32
    w = np.zeros((NSLOT_PER_SEQ, 128, SLOT_ELEMS), dtype=f)
    i = 0
    w_in = np.asarray(inp["rg_w_in"], f)[0]
    w_a = np.asarray(inp["rg_w_a"], f)[0]
    w_x = np.asarray(inp["rg_w_x"], f)[0]
    win_r = w_in.reshape(8, 128, 2, 8, 128)
    for c in range(8):
        blk = win_r[:, :, :, c, :]
        w[i, :, 0:2048] = blk.transpose(1, 0, 2, 3).reshape(128, 2048)
        g = np.stack([w_a[0, c], w_x[0, c], w_a[1, c], w_x[1, c]], axis=1)
        w[i, :, 2048:2560] = g.reshape(128, 512)
        i += 1

    def proj_slots(W):
        nonlocal i
        Wr = W.reshape(8, 128, 2, 512)
        for s2 in range(2):
            w[i] = Wr[:, :, s2, :].transpose(1, 0, 2).reshape(128, 4096)
            i += 1

    def mlp_slots(Wu, Wd):
        nonlocal i
        Wur = Wu.reshape(8, 128, 8, 512)
        Wdr = Wd.reshape(8, 4, 128, 1024)
        for g in range(8):
            w[i] = Wur[:, :, g, :].transpose(1, 0, 2).reshape(128, 4096)
            i += 1
            w[i] = Wdr[g].transpose(1, 0, 2).reshape(128, 4096)
            i += 1

    proj_slots(np.asarray(inp["rg_w_out"], f)[0])
    mlp_slots(np.asarray(inp["mlp_w_up"], f)[0], np.asarray(inp["mlp_w_down"], f)[0])
    wqkv = np.asarray(inp["at_w_qkv"], f)[0]
    wr = wqkv.reshape(8, 128, 3, 512)
    for s3 in (2, 0, 1):
        w[i] = wr[:, :, s3, :].transpose(1, 0, 2).reshape(128, 4096)
        i += 1
    proj_slots(np.asarray(inp["at_w_o"], f)[0])
    mlp_slots(np.asarray(inp["mlp_w_up"], f)[1], np.asarray(inp["mlp_w_down"], f)[1])
    assert i == NSLOT_PER_SEQ
    return w


def prep_consts(inp):
    f = np.float32
    c = np.zeros((128, NCONST), dtype=f)

    def chunks(v):
        return np.asarray(v, f).reshape(8, 128).T

    for l in range(2):
        c[:, C_MIXG + 8 * l:C_MIXG + 8 * l + 8] = chunks(inp["norm_mix_g"][l])
        c[:, C_MLPG + 8 * l:C_MLPG + 8 * l + 8] = chunks(inp["norm_mlp_g"][l])
    c[:, C_FING:C_FING + 8] = chunks(inp["final_g"])
    cw = np.asarray(inp["rg_conv_w"], f)[0, :, 0, :]
    for j in range(4):
        c[:, C_CONVW + 8 * j:C_CONVW + 8 * j + 8] = chunks(cw[j])
    c[:, C_CONVB:C_CONVB + 8] = chunks(np.asarray(inp["rg_conv_b"], f)[0])
    for d in range(2):
        c[:, C_BA + 8 * d:C_BA + 8 * d + 8] = chunks(np.asarray(inp["rg_b_a"], f)[0, d])
        c[:, C_BX + 8 * d:C_BX + 8 * d + 8] = chunks(np.asarray(inp["rg_b_x"], f)[0, d])
        c[:, C_LAM + 8 * d:C_LAM + 8 * d + 8] = chunks(np.asarray(inp["rg_lam"], f)[0, d])
    c[:, C_QG] = np.asarray(inp["at_q_g"], f)[0]
    c[:, C_KG] = np.asarray(inp["at_k_g"], f)[0]
    return c


def prep_static():
    f = np.float32
    t = np.arange(L)
    row = (t // 64).astype(f)
    colp = (t % 64).astype(f)
    inv = (np.float32(10000.0) ** (-np.arange(32, dtype=f) / np.float32(32))).astype(f)
    rope = np.zeros((2, 128, L), dtype=f)
    for p in range(128):
        pos = row if p < 64 else colp
        ang = (pos * inv[p % 32]).astype(f)
        rope[0, p] = np.cos(ang)
        rope[1, p] = np.sin(ang)
    mats = np.zeros((128, 256), dtype=f)
    mats[:, 0:128] = 1.0
    for m in range(128):
        if m % 64 < 32:
            mats[m + 32, 128 + m] = -1.0
        else:
            mats[m - 32, 128 + m] = 1.0
    return rope, mats


_CACHE = {}


def kernel(**inputs):
    ncores = 8
    nseq = 2
    x = np.asarray(inputs["x"], np.float32)
    xT = np.ascontiguousarray(x.transpose(0, 2, 1))
    wst = prep_weights(inputs)
    cst = prep_consts(inputs)
    rope, mats = prep_static()
    if "nc" not in _CACHE:
        _CACHE["nc"] = build_program(nseq=nseq)
    nc = _CACHE["nc"]
    in_maps = []
    for i in range(ncores):
        in_maps.append({"xT": xT[i * nseq:(i + 1) * nseq], "wst": wst, "cst": cst, "rope": rope, "mats": mats})
    res = run_bass_kernel_spmd(nc, in_maps, core_ids=list(range(ncores)))
    outT = np.concatenate([np.asarray(r["outT"]) for r in res.results], axis=0)
    return np.ascontiguousarray(outT.transpose(0, 2, 1)).astype(np.float32)
```

```python
import math
from contextlib import ExitStack

import numpy as np
import concourse.bass as bass
import concourse.mybir as mybir
from concourse.bass_utils import run_bass_kernel_spmd

F32 = mybir.dt.float32
BF16 = mybir.dt.bfloat16
AF = mybir.ActivationFunctionType
ALU = mybir.AluOpType

D = 1024
L = 2048
NCH = 8
TB = 512
NTB = L // TB
EPS = 1e-6
NSLOT_PER_SEQ = 47
FENCE = False
SYNC_SAME = True
SLOT_ELEMS = 4096


class Buf:
    __slots__ = ("name", "w", "r", "dsem", "dcnt")

    def __init__(self, name):
        self.name = name
        self.w = None
        self.r = {}
        self.dsem = None
        self.dcnt = 0


class Prog:
    ENGS = ("pe", "act", "dve", "pool", "sp")

    def __init__(self, nc, stack, sync_same=True):
        self.nc = nc
        self.stack = stack
        self.sync_same = sync_same
        self.q = {e: [] for e in self.ENGS}
        self.sem = {e: stack.enter_context(nc.semaphore("s_" + e)) for e in self.ENGS}
        self.cnt = {e: 0 for e in self.ENGS}
        self.seen = {e: {} for e in self.ENGS}
        self.nsem = 0
        self.dma_toks = []

    def _waits(self, engine, deps, skip_sem=None):
        out = []
        seen = self.seen[engine]
        for d in deps:
            if d is None:
                continue
            kind, key, n = d
            if kind == "eng":
                k = ("e", key)
                sem = self.sem[key]
            else:
                if skip_sem is not None and key is skip_sem:
                    continue
                k = ("d", key.num)
                sem = key
            if seen.get(k, 0) >= n:
                continue
            seen[k] = n
            out.append((sem, n))
        return out

    @staticmethod
    def _deps(reads, writes):
        deps = []
        for b in reads:
            if b.w is not None:
                deps.append(b.w)
        for b in writes:
            if b.w is not None:
                deps.append(b.w)
            deps.extend(b.r.values())
        return deps

    def op(self, engine, emit, reads=(), writes=(), same=None):
        deps = self._deps(reads, writes)
        if same is None:
            same = self.sync_same
        if not same:
            deps = [d for d in deps if not (d[0] == "eng" and d[1] == engine)]
        waits = self._waits(engine, deps)
        self.cnt[engine] += 1
        tok = ("eng", engine, self.cnt[engine])
        sem = self.sem[engine]

        def run(eng, waits=waits, emit=emit, sem=sem):
            for s, v in waits:
                eng.wait_ge(s, v)
            emit(eng).then_inc(sem, 1)

        self.q[engine].append(run)
        for b in reads:
            b.r[("e", engine)] = tok
        for b in writes:
            b.w = tok
            b.r = {}
        return tok

    def dma(self, queue, out_ap, in_ap, reads=(), writes=(), owner=None, **kw):
        if owner is None:
            owner = writes[0] if writes else reads[0]
        if owner.dsem is None:
            owner.dsem = self.stack.enter_context(self.nc.semaphore("d%d" % self.nsem))
            self.nsem += 1
        deps = self._deps(reads, writes)
        waits = self._waits(queue, deps, skip_sem=owner.dsem)
        owner.dcnt += 16
        tok = ("dma", owner.dsem, owner.dcnt)
        sem = owner.dsem

        def run(eng, waits=waits, sem=sem, out_ap=out_ap, in_ap=in_ap, kw=kw):
            for s, v in waits:
                eng.wait_ge(s, v)
            eng.dma_start(out=out_ap, in_=in_ap, **kw).then_inc(sem, 16)

        self.q[queue].append(run)
        for b in reads:
            b.r[("d", sem.num)] = tok
        for b in writes:
            b.w = tok
            b.r = {}
        self.dma_toks.append(tok)
        return tok

    def emit_all(self):
        nc = self.nc
        last = {}
        for t in self.dma_toks:
            last[t[1].num] = t
        toks = list(last.values()) + [("eng", e, self.cnt[e]) for e in self.ENGS
                                      if self.cnt[e] > 0 and e != "sp"]
        waits = self._waits("sp", toks)

        def fin(eng, waits=waits):
            for s, v in waits:
                eng.wait_ge(s, v)

        self.q["sp"].append(fin)
        q = self.q
        with nc.Block() as block:
            @block.tensor
            def _(eng):
                for f in q["pe"]:
                    f(eng)

            @block.scalar
            def _(eng):
                for f in q["act"]:
                    f(eng)

            @block.vector
            def _(eng):
                for f in q["dve"]:
                    f(eng)

            @block.gpsimd
            def _(eng):
                for f in q["pool"]:
                    f(eng)

            @block.sync
            def _(eng):
                for f in q["sp"]:
                    f(eng)


C_MIXG = 0
C_MLPG = 16
C_FING = 32
C_CONVW = 40
C_CONVB = 72
C_BA = 80
C_BX = 96
C_LAM = 112
C_QG = 128
C_KG = 129
NCONST = 130


def build_program(nseq=2, stop_after=4):
    nc = bass.Bass("TRN2", target_bir_lowering=False)
    xT_d = nc.dram_tensor("xT", [nseq, D, L], F32, kind="ExternalInput").ap()
    wst_d = nc.dram_tensor("wst", [NSLOT_PER_SEQ, 128, SLOT_ELEMS], F32, kind="ExternalInput").ap()
    cst_d = nc.dram_tensor("cst", [128, NCONST], F32, kind="ExternalInput").ap()
    rope_d = nc.dram_tensor("rope", [2, 128, L], F32, kind="ExternalInput").ap()
    mats_d = nc.dram_tensor("mats", [128, 256], F32, kind="ExternalInput").ap()
    out_d = nc.dram_tensor("outT", [nseq, D, L], F32, kind="ExternalOutput").ap()

    with ExitStack() as st:
        P = Prog(nc, st, sync_same=SYNC_SAME)
        sb = lambda n, s, d: st.enter_context(nc.sbuf_tensor(n, s, d))

        xt = sb("xt", [128, NCH * L], F32)
        ht = sb("ht", [128, NCH * L], BF16)
        yt = sb("yt", [128, NCH * L], BF16)
        Bx = [[Buf("x%d_%d" % (c, t)) for t in range(NTB)] for c in range(NCH)]
        Bh = [[Buf("h%d_%d" % (c, t)) for t in range(NTB)] for c in range(NCH)]
        By = [[Buf("y%d_%d" % (c, t)) for t in range(NTB)] for c in range(NCH)]

        def xv(c, t):
            return xt[:, c * L + t * TB: c * L + (t + 1) * TB]

        def hv(c, t):
            return ht[:, c * L + t * TB: c * L + (t + 1) * TB]

        def yv(c, t):
            return yt[:, c * L + t * TB: c * L + (t + 1) * TB]

        NRING = 4
        ring = [sb("ring%d" % i, [128, SLOT_ELEMS], BF16) for i in range(NRING)]
        Bring = [Buf("ring%d" % i) for i in range(NRING)]
        ring_n = [0]

        T8 = [sb("T8_%d" % i, [128, L], F32) for i in range(3)]
        BT8 = [[Buf("T8_%d_%d" % (i, j)) for j in range(8 if i < 2 else 4)] for i in range(3)]
        T4 = [sb("T4_%d" % i, [128, L], BF16) for i in range(2)]
        BT4 = [[Buf("T4_%d_%d" % (i, j)) for j in range(4)] for i in range(2)]
        NS = 6
        S = [sb("S%d" % i, [128, TB], F32) for i in range(NS)]
        BS = [Buf("S%d" % i) for i in range(NS)]

        cst = sb("cst_sb", [128, NCONST], F32)
        Bcst = Buf("cst")
        der = sb("der", [128, 64], F32)
        Bder = Buf("der")
        mats = sb("mats_sb", [128, 256], BF16)
        Bmats = Buf("mats")

        ps = st.enter_context(nc.psum_tensor("ps", [128, 8 * 512], F32))
        Bps = [Buf("ps%d" % i) for i in range(8)]

        def bank(i):
            return ps[:, i * 512:(i + 1) * 512]

        poolA = [0]
        poolB = [0]

        def nextA():
            i = poolA[0] % 4
            poolA[0] += 1
            return i

        def nextB():
            i = 4 + poolB[0] % 4
            poolB[0] += 1
            return i

        P.dma("sp", cst[:], cst_d[:], writes=[Bcst])
        P.dma("pool", mats[:], mats_d[:], writes=[Bmats])
        ones = mats[:, 0:128]
        rotT = mats[:, 128:256]
        P.op("act", lambda e: e.activation(der[:, 0:16], cst[:, C_LAM:C_LAM + 16], AF.Exp, scale=-1.0),
             reads=[Bcst], writes=[Bder])
        P.op("act", lambda e: e.activation(der[:, 0:16], der[:, 0:16], AF.Ln, bias=1.0, scale=1.0),
             reads=[Bder], writes=[Bder])
        P.op("dve", lambda e: e.tensor_scalar(der[:, 0:16], der[:, 0:16], -4.0, None, ALU.mult),
             reads=[Bder], writes=[Bder])
        P.op("dve", lambda e: e.tensor_scalar(der[:, 16:48], cst[:, C_BA:C_BA + 32], 0.5, None, ALU.mult),
             reads=[Bcst, Bder], writes=[Bder])

        def col(i):
            return cst[:, i:i + 1]

        def dcol(i):
            return der[:, i:i + 1]

        fence = Buf("fence")
        Bdbg = Buf("dbg")

        def wload(idx, nelem=SLOT_ELEMS):
            j = ring_n[0] % NRING
            ring_n[0] += 1
            P.dma("pool", ring[j][:, 0:nelem], wst_d[idx, :, 0:nelem], reads=[fence], writes=[Bring[j]], owner=Bring[j],
                  max_dma_last_dim=8192)
            return ring[j], Bring[j]

        sq_n = [0]

        def rmsnorm_tb(t, gbase, out_fn=None):
            pb = nextB()
            for c in range(NCH):
                k = sq_n[0] % 4
                sq_n[0] += 1
                sqv = T8[2][:].bitcast(BF16)[:, k * 1024:k * 1024 + TB]
                P.op("act", lambda e, c=c, sqv=sqv: e.activation(sqv, xv(c, t), AF.Square),
                     reads=[Bx[c][t]], writes=[BT8[2][k]])
                P.op("pe", lambda e, c=c, sqv=sqv, pb=pb: e.matmul(bank(pb), ones, sqv, start=(c == 0), stop=(c == NCH - 1)),
                     reads=[BT8[2][k], Bmats], writes=[Bps[pb]], same=False)
            si = 4 + (t % 2)
            P.op("act", lambda e, pb=pb, si=si: e.activation(S[si][:], bank(pb), AF.Ln, bias=dcol(48), scale=1.0 / D),
                 reads=[Bps[pb], Bder], writes=[BS[si]])
            P.op("act", lambda e, si=si: e.activation(S[si][:], S[si][:], AF.Exp, scale=-0.5),
                 reads=[BS[si]], writes=[BS[si]])
            for c in range(NCH):
                if out_fn is None:
                    P.op("dve", lambda e, c=c, si=si: e.scalar_tensor_tensor(hv(c, t), xv(c, t), col(gbase + c), S[si][:], ALU.mult, ALU.mult),
                         reads=[Bx[c][t], BS[si], Bcst], writes=[Bh[c][t]])
                else:
                    out_fn(c, t, si)

        P.op("dve", lambda e: e.memset(der[:, 48:49], EPS), reads=[], writes=[Bder])

        for s in range(nseq):
            wbase = 0
            if FENCE and s > 0:
                fence.w = ("eng", "dve", P.cnt["dve"])
            def load_x(s_, t):
                for c in range(NCH):
                    P.dma("sp", xv(c, t), xT_d[s_, c * 128:(c + 1) * 128, t * TB:(t + 1) * TB], writes=[Bx[c][t]])
            if s == 0 or stop_after < 4:
                for t in range(NTB):
                    load_x(s, t)

            for t in range(NTB):
                if s > 0 and stop_after >= 4 and t < NTB - 1:
                    continue
                rmsnorm_tb(t, C_MIXG + 0)

            T4f = [T4[0][:].bitcast(F32), T4[1][:].bitcast(F32)]
            UT = [(S[k][:], [BS[k]]) for k in range(NS)]
            for i in range(2):
                for m in range(2):
                    UT.append((T4f[i][:, m * TB:(m + 1) * TB], [BT4[i][2 * m], BT4[i][2 * m + 1]]))
            for m in range(4):
                UT.append((T8[2][:, m * TB:(m + 1) * TB], [BT8[2][m]]))
            ut_n = [0, 0, 0]
            UTm, UTau = UT[0:2], UT[2:10]
            UTacc = [(T8[2][:, 0:2 * TB], [BT8[2][0], BT8[2][1]]), (T8[2][:, 2 * TB:4 * TB], [BT8[2][2], BT8[2][3]])]

            def ut_next(kind=2):
                pool_ = (UTm, UTacc, UTau)[kind]
                r = pool_[ut_n[kind] % len(pool_)]
                ut_n[kind] += 1
                return r

            rb = [T8[0][:].bitcast(BF16), T8[1][:].bitcast(BF16)]
            Rg = ps[:, 0:L]
            wts = {}

            def rec_mm(c):
                wt, Bw = wload(wbase + c, 2560)
                wts[c] = (wt, Bw)
                for t in range(NTB):
                    def mm_rec(e, t=t, wt=wt):
                        ins = None
                        for k in range(NCH):
                            ins = e.matmul(bank(t), wt[:, k * 256 + 128:k * 256 + 256], hv(k, t), start=(k == 0), stop=(k == NCH - 1))
                        return ins
                    P.op("pe", mm_rec, reads=[Bw] + [Bh[k][t] for k in range(NCH)], writes=[Bps[t]], same=False)

            gate_pb = {}

            def gate_mm(c):
                wt, Bw = wts[c]
                for t in range(NTB):
                    pb = nextB()
                    gate_pb[(c, t)] = pb

                    def mm_gate(e, t=t, wt=wt, pb=pb):
                        ins = None
                        for k in range(NCH):
                            ins = e.matmul(bank(pb), wt[:, k * 256:k * 256 + 128], hv(k, t), start=(k == 0), stop=(k == NCH - 1))
                        return ins
                    P.op("pe", mm_gate, reads=[Bw] + [Bh[k][t] for k in range(NCH)], writes=[Bps[pb]], same=False)

            def gelu(c):
                for t in range(NTB):
                    pb = gate_pb.pop((c, t))
                    P.op("act", lambda e, t=t, pb=pb, c=c: e.activation(yv(c, t), bank(pb), AF.Gelu_apprx_tanh),
                         reads=[Bps[pb]], writes=[By[c][t]])

            conv_acc = {}

            def conv(c):
                W2 = 2 * TB
                for t2 in range(2):
                    acc, Bacc = ut_next(1)
                    conv_acc[(c, t2)] = (acc, Bacc)
                    base = t2 * W2
                    rps = [Bps[b] for b in ((0, 1, 2) if t2 == 0 else (1, 2, 3))]
                    P.op("act", lambda e, c=c, acc=acc, base=base: e.activation(acc, Rg[:, base:base + W2], AF.Identity, bias=col(C_CONVB + c), scale=col(C_CONVW + 2 * 8 + c)),
                         reads=rps + [Bcst], writes=Bacc)
                    lo = 1 if t2 == 0 else 0
                    P.op("dve", lambda e, c=c, acc=acc, base=base, lo=lo: e.scalar_tensor_tensor(acc[:, lo:W2], Rg[:, base + lo - 1:base + W2 - 1], col(C_CONVW + 1 * 8 + c), acc[:, lo:W2], ALU.mult, ALU.add),
                         reads=rps + [Bcst] + Bacc, writes=Bacc)
                    lo = 2 if t2 == 0 else 0
                    P.op("dve", lambda e, c=c, acc=acc, base=base, lo=lo: e.scalar_tensor_tensor(acc[:, lo:W2], Rg[:, base + lo - 2:base + W2 - 2], col(C_CONVW + 0 * 8 + c), acc[:, lo:W2], ALU.mult, ALU.add),
                         reads=rps + [Bcst] + Bacc, writes=Bacc)
                    hi = W2 - 1 if t2 == 1 else W2
                    i2 = c % 2
                    Brb = [BT8[i2][2 * t2], BT8[i2][2 * t2 + 1]]
                    P.op("dve", lambda e, c=c, acc=acc, base=base, hi=hi, i2=i2: e.scalar_tensor_tensor(rb[i2][:, base:base + hi], Rg[:, base + 1:base + 1 + hi], col(C_CONVW + 3 * 8 + c), acc[:, 0:hi], ALU.mult, ALU.add),
                         reads=rps + [Bcst] + Bacc, writes=Brb)
                    if t2 == 1:
                        P.op("dve", lambda e, acc=acc, i2=i2: e.tensor_copy(rb[i2][:, L - 1:L], acc[:, W2 - 1:W2]),
                             reads=Bacc + Brb, writes=Brb)

            def conv_cast(c):
                W2 = 2 * TB
                i2 = c % 2
                for t2 in range(2):
                    conv_acc.pop((c, t2))

            ustate = {}

            def unit_A(c, d, t):
                i2 = c % 2
                wt, Bw = wts[c]
                recv = rb[i2][:, t * TB:(t + 1) * TB]
                Brec = BT8[i2][t]
                (A, BA), (U, BU) = ut_next(), ut_next()
                pa = nextB()
                P.op("pe", lambda e, d=d, pa=pa, wt=wt, recv=recv: e.matmul(bank(pa), wt[:, 2048 + (d * 2 + 0) * 128:2048 + (d * 2 + 1) * 128], recv, start=True, stop=True),
                     reads=[Bw, Brec], writes=[Bps[pa]], same=False)
                P.op("act", lambda e, d=d, c=c, pa=pa, A=A: e.activation(A, bank(pa), AF.Tanh, bias=dcol(16 + d * 8 + c), scale=0.5),
                     reads=[Bps[pa], Bder], writes=BA)
                px = nextB()
                P.op("pe", lambda e, d=d, px=px, wt=wt, recv=recv: e.matmul(bank(px), wt[:, 2048 + (d * 2 + 1) * 128:2048 + (d * 2 + 2) * 128], recv, start=True, stop=True),
                     reads=[Bw, Brec], writes=[Bps[px]], same=False)
                P.op("act", lambda e, d=d, c=c, px=px, U=U: e.activation(U, bank(px), AF.Tanh, bias=dcol(32 + d * 8 + c), scale=0.5),
                     reads=[Bps[px], Bder], writes=BU)
                P.op("act", lambda e, d=d, c=c, A=A: e.activation(A, A, AF.Exp, bias=dcol(d * 8 + c), scale=dcol(d * 8 + c)),
                     reads=BA + [Bder], writes=BA)
                ustate[(c, d, t)] = (A, BA, U, BU, recv, Brec)

            def unit_B(c, d, t, prev):
                i2 = c % 2
                A, BA, U, BU, recv, Brec = ustate[(c, d, t)]
                M, BM = ut_next(0)
                P.op("act", lambda e, A=A, M=M: e.activation(M, A, AF.Square),
                     reads=BA, writes=BM)
                P.op("act", lambda e, M=M: e.activation(M, M, AF.Sqrt, bias=1.0, scale=-1.0),
                     reads=BM, writes=BM)
                P.op("dve", lambda e, U=U, M=M: e.scalar_tensor_tensor(U, U, 1.0, M, ALU.add, ALU.mult),
                     reads=BU + BM, writes=BU)
                P.op("dve", lambda e, U=U, recv=recv: e.scalar_tensor_tensor(U, U, 0.5, recv, ALU.mult, ALU.mult),
                     reads=BU + [Brec], writes=BU)
                stv = rb[i2][:, L + t * TB:L + (t + 1) * TB]
                Bst = BT8[i2][4 + t]
                if d == 0:
                    if prev is None:
                        init, Bin = 0.0, []
                    else:
                        tp_ = prev[2]
                        init, Bin = rb[i2][:, L + tp_ * TB + TB - 1:L + tp_ * TB + TB], [BT8[i2][4 + tp_]]
                    P.op("dve", lambda e, U=U, A=A, init=init, stv=stv: e.tensor_tensor_scan(stv, A, U, init, ALU.mult, ALU.add),
                         reads=BA + BU + Bin, writes=[Bst])
                else:
                    if prev is None:
                        init, Bin = 0.0, []
                    else:
                        init, Bin = ustate[prev][2][:, 0:1], ustate[prev][3]
                    P.op("dve", lambda e, U=U, A=A, init=init: e.tensor_tensor_scan(U[:, ::-1], A[:, ::-1], U[:, ::-1], init, ALU.mult, ALU.add),
                         reads=BA + BU + Bin, writes=BU)
                    P.op("dve", lambda e, U=U, A=A, stv=stv: e.tensor_tensor(A, U, stv, ALU.add),
                         reads=BU + [Bst], writes=BA)
                    P.op("dve", lambda e, A=A, c=c, t=t: e.tensor_tensor(yv(c, t), A, yv(c, t), ALU.mult),
                         reads=BA + [By[c][t]], writes=[By[c][t]])
                if prev is not None:
                    del ustate[prev]

            rec_mm(0)
            conv(0)
            conv_cast(0)
            gate_mm(0)
            gelu(0)
            for c in range(NCH):
                order0 = [0, 1, 2, 3]
                order1 = [3, 2, 1, 0]
                for t in order0:
                    unit_A(c, 0, t)
                if c + 1 < NCH:
                    rec_mm(c + 1)
                prev = None
                for t in order0:
                    unit_B(c, 0, t, prev)
                    prev = (c, 0, t)
                last0 = prev
                if c + 1 < NCH:
                    conv(c + 1)
                for t in order1:
                    unit_A(c, 1, t)
                if c + 1 < NCH:
                    conv_cast(c + 1)
                    gate_mm(c + 1)
                prev = None
                for t in order1:
                    unit_B(c, 1, t, prev)
                    prev = (c, 1, t)
                del ustate[last0]
                del ustate[prev]
                if c + 1 < NCH:
                    gelu(c + 1)
            wbase += 8

            def out_proj(widx, after_tb=None):
                for s2 in range(2):
                    wt, Bw = wload(widx + s2)
                    for t in range(NTB):
                        for dd in range(4):
                            pb = nextB()

                            def mm(e, wt=wt, t=t, dd=dd, pb=pb):
                                ins = None
                                for c in range(NCH):
                                    ins = e.matmul(bank(pb), wt[:, c * 512 + dd * 128:c * 512 + (dd + 1) * 128], yv(c, t), start=(c == 0), stop=(c == NCH - 1))
                                return ins
                            P.op("pe", mm, reads=[Bw] + [By[c][t] for c in range(NCH)], writes=[Bps[pb]], same=False)
                            dc = s2 * 4 + dd
                            P.op("dve", lambda e, dc=dc, t=t, pb=pb: e.tensor_tensor(xv(dc, t), xv(dc, t), bank(pb), ALU.add),
                                 reads=[Bx[dc][t], Bps[pb]], writes=[Bx[dc][t]])
                        if s2 == 1 and after_tb is not None:
                            after_tb(t)

            def mlp(widx, gbase, do_norm=True, after_tb=None):
                if do_norm:
                    for t in range(NTB):
                        rmsnorm_tb(t, gbase)
                items = [(g, t) for g in range(8) for t in range(NTB)]
                slots = {}
                actbuf = [T4[0], T4[1]]
                Bact = [BT4[0], BT4[1]]

                def up(i):
                    g, t = items[i]
                    if t == 0:
                        slots[g] = (wload(widx + 2 * g), wload(widx + 2 * g + 1))
                    (wu, Bwu), _ = slots[g]
                    ab = i % 2
                    for f in range(4):
                        pa = nextA()

                        def mm(e, wu=wu, f=f, t=t, pa=pa):
                            ins = None
                            for k in range(NCH):
                                ins = e.matmul(bank(pa), wu[:, k * 512 + f * 128:k * 512 + (f + 1) * 128], hv(k, t), start=(k == 0), stop=(k == NCH - 1))
                            return ins
                        P.op("pe", mm, reads=[Bwu] + [Bh[k][t] for k in range(NCH)], writes=[Bps[pa]], same=False)
                        si = f % 4
                        P.op("act", lambda e, pa=pa, si=si: e.activation(S[si][:], bank(pa), AF.Relu),
                             reads=[Bps[pa]], writes=[BS[si]])
                        P.op("dve", lambda e, si=si, ab=ab, f=f: e.tensor_tensor(actbuf[ab][:, f * TB:(f + 1) * TB], S[si][:], S[si][:], ALU.mult),
                             reads=[BS[si]], writes=[Bact[ab][f]])

                def down(i):
                    g, t = items[i]
                    _, (wd, Bwd) = slots[g]
                    ab = i % 2
                    for dc in range(NCH):
                        pb = nextB()

                        def mm(e, wd=wd, dc=dc, pb=pb, ab=ab):
                            ins = None
                            for f in range(4):
                                ins = e.matmul(bank(pb), wd[:, f * 1024 + dc * 128:f * 1024 + (dc + 1) * 128], actbuf[ab][:, f * TB:(f + 1) * TB], start=(f == 0), stop=(f == 3))
                            return ins
                        P.op("pe", mm, reads=[Bwd] + Bact[ab], writes=[Bps[pb]], same=False)
                        P.op("dve", lambda e, dc=dc, t=t, pb=pb: e.tensor_tensor(xv(dc, t), xv(dc, t), bank(pb), ALU.add),
                             reads=[Bx[dc][t], Bps[pb]], writes=[Bx[dc][t]])
                    if g == 7 and after_tb is not None:
                        after_tb(t)

                for i in range(len(items) + 1):
                    if i < len(items):
                        up(i)
                    if i >= 1:
                        down(i - 1)

            if stop_after >= 2:
                out_proj(wbase, after_tb=lambda t: rmsnorm_tb(t, C_MLPG + 0))
            else:
                out_proj(wbase)
            wbase += 2
            if stop_after >= 3:
                mlp(wbase, C_MLPG + 0, do_norm=False, after_tb=lambda t: rmsnorm_tb(t, C_MIXG + 8))
            elif stop_after >= 2:
                mlp(wbase, C_MLPG + 0, do_norm=False)
            wbase += 16

            if stop_after >= 3:
                wkv, Bwkv = wload(wbase + 0)
                wq = [wload(wbase + 1), wload(wbase + 2)]
                kT = T8[0][:].bitcast(BF16)
                BkT = BT8[0]
                Vt = T8[1][:].bitcast(BF16)
                BV = BT8[1]
                cs_t = [(T8[2][:, 0:TB], T8[2][:, TB:2 * TB], BT8[2][0], BT8[2][1]),
                        (T8[2][:, 2 * TB:3 * TB], T8[2][:, 3 * TB:4 * TB], BT8[2][2], BT8[2][3])]
                pitems = []
                for t in range(NTB):
                    for j in range(10):
                        pitems.append((t, j))
                    for tt in range(4):
                        pitems.append((t, 10 + tt))
                pstate = {}

                def p1(i):
                    t, j = pitems[i]
                    if j == 0:
                        cosv, sinv, Bcos, Bsin = cs_t[t % 2]
                        P.dma("sp", cosv, rope_d[0, :, t * TB:(t + 1) * TB], writes=[Bcos])
                        P.dma("sp", sinv, rope_d[1, :, t * TB:(t + 1) * TB], writes=[Bsin])
                    if j >= 10:
                        tile_i = t * 4 + (j - 10)
                        pa = nextA()

                        def mmv(e, pa=pa, tile_i=tile_i, wkv=wkv):
                            ins = None
                            for k in range(NCH):
                                ins = e.matmul(ps[:, pa * 512:pa * 512 + 256], ht[:, k * L + tile_i * 128:k * L + (tile_i + 1) * 128], wkv[:, k * 512 + 256:k * 512 + 512], start=(k == 0), stop=(k == NCH - 1))
                            return ins
                        P.op("pe", mmv, reads=[Bwkv] + [Bh[k][t] for k in range(NCH)], writes=[Bps[pa]], same=False)
                        P.op("act", lambda e, pa=pa, tile_i=tile_i: e.activation(Vt[:, tile_i * 256:(tile_i + 1) * 256], ps[:, pa * 512:pa * 512 + 256], AF.Copy),
                             reads=[Bps[pa]], writes=[BV[tile_i // 2]])
                        return
                    if j < 2:
                        wt, Bw, cofs, gcolm = wkv, Bwkv, j * 128, C_KG
                        dest = kT[:, j * L + t * TB: j * L + (t + 1) * TB]
                        Bdest = BkT[(j * L + t * TB) // 512]
                    else:
                        hq = j - 2
                        (wt, Bw) = wq[hq // 4]
                        cofs, gcolm = (hq % 4) * 128, C_QG
                        dest = yv(hq, t)
                        Bdest = By[hq][t]
                    pa = nextA()

                    def mm(e, wt=wt, cofs=cofs, t=t, pa=pa):
                        ins = None
                        for k in range(NCH):
                            ins = e.matmul(bank(pa), wt[:, k * 512 + cofs:k * 512 + cofs + 128], hv(k, t), start=(k == 0), stop=(k == NCH - 1))
                        return ins
                    P.op("pe", mm, reads=[Bw] + [Bh[k][t] for k in range(NCH)], writes=[Bps[pa]], same=False)
                    kk = qn_n[0] % 4
                    qn_n[0] += 1
                    sqv = T4[0][:, kk * TB:(kk + 1) * TB]
                    P.op("act", lambda e, sqv=sqv, pa=pa: e.activation(sqv, bank(pa), AF.Square),
                         reads=[Bps[pa]], writes=[BT4[0][kk]])
                    pstate[i] = dict(pa=pa, kk=kk, sqv=sqv, gcolm=gcolm, dest=dest, Bdest=Bdest)

                def p2(i):
                    if i not in pstate:
                        return
                    d = pstate[i]
                    pa, kk, sqv, gcolm = d["pa"], d["kk"], d["sqv"], d["gcolm"]
                    qnv = T4[1][:, kk * TB:(kk + 1) * TB]
                    pb = nextB()
                    P.op("pe", lambda e, pb=pb, sqv=sqv: e.matmul(bank(pb), ones, sqv, start=True, stop=True),
                         reads=[BT4[0][kk], Bmats], writes=[Bps[pb]], same=False)
                    si = kk % 2
                    P.op("act", lambda e, pb=pb, si=si: e.activation(S[si][:], bank(pb), AF.Ln, bias=dcol(48), scale=1.0 / 128),
                         reads=[Bps[pb], Bder], writes=[BS[si]])
                    P.op("act", lambda e, si=si: e.activation(S[si][:], S[si][:], AF.Exp, scale=-0.5),
                         reads=[BS[si]], writes=[BS[si]])
                    P.op("dve", lambda e, qnv=qnv, pa=pa, si=si, gcolm=gcolm: e.scalar_tensor_tensor(qnv, bank(pa), col(gcolm), S[si][:], ALU.mult, ALU.mult),
                         reads=[Bps[pa], BS[si], Bcst], writes=[BT4[1][kk]])
                    d["qnv"] = qnv

                def p3(i):
                    if i not in pstate:
                        return
                    d = pstate.pop(i)
                    t, j = pitems[i]
                    cosv, sinv, Bcos, Bsin = cs_t[t % 2]
                    kk, qnv, dest, Bdest = d["kk"], d["qnv"], d["dest"], d["Bdest"]
                    pr = nextB()
                    P.op("pe", lambda e, pr=pr, qnv=qnv: e.matmul(bank(pr), rotT, qnv, start=True, stop=True),
                         reads=[BT4[1][kk], Bmats], writes=[Bps[pr]], same=False)
                    s1, s2 = 2 + 2 * (kk % 2), 3 + 2 * (kk % 2)
                    P.op("dve", lambda e, qnv=qnv, s1=s1, cosv=cosv: e.tensor_tensor(S[s1][:], qnv, cosv, ALU.mult),
                         reads=[BT4[1][kk], Bcos], writes=[BS[s1]])
                    P.op("dve", lambda e, pr=pr, s2=s2, sinv=sinv: e.tensor_tensor(S[s2][:], bank(pr), sinv, ALU.mult),
                         reads=[Bps[pr], Bsin], writes=[BS[s2]])
                    P.op("dve", lambda e, dest=dest, s1=s1, s2=s2: e.tensor_tensor(dest, S[s1][:], S[s2][:], ALU.add),
                         reads=[BS[s1], BS[s2]], writes=[Bdest])

                qn_n = [0]
                NP = len(pitems)
                for i in range(NP + 3):
                    if i < NP:
                        p1(i)
                    if 1 <= i < NP + 1:
                        p2(i - 1)
                    if i >= 3:
                        p3(i - 3)
                wbase_att = wbase + 3

                scale = 1.0 / math.sqrt(128.0)
                items = [(h, qb, kt) for h in range(8) for qb in range(NTB) for kt in range(16)]
                PT8 = [(T4[i][:, k * TB:(k + 1) * TB], BT4[i][k]) for i in range(2) for k in range(4)]
                T8b2 = T8[2][:].bitcast(BF16)
                Fsm = [Buf("sm%d" % j) for j in range(8)]
                for j in range(8):
                    Fsm[j].w = BT8[2][j // 2].w
                    Fsm[j].r = dict(BT8[2][j // 2].r)
                SMT = [(T8b2[:, j * TB:(j + 1) * TB], Fsm[j]) for j in range(8)]
                pt_n = [0]
                sm_n = [0]
                state = {}
                smstate = {}

                def s_part(i):
                    h, qb, kt = items[i]
                    kv = h // 4
                    pa = nextA()
                    kcol = kv * L + kt * 128
                    P.op("pe", lambda e, pa=pa, kcol=kcol, h=h, qb=qb: e.matmul(bank(pa), kT[:, kcol:kcol + 128], yv(h, qb), start=True, stop=True),
                         reads=[BkT[kcol // 512], By[h][qb]], writes=[Bps[pa]], same=False)
                    ptv, Bpt = PT8[pt_n[0] % 8]
                    pt_n[0] += 1
                    P.op("act", lambda e, pa=pa, ptv=ptv: e.activation(ptv, bank(pa), AF.Exp, scale=scale),
                         reads=[Bps[pa]], writes=[Bpt])
                    state[i] = (ptv, Bpt)

                def pv_part(i):
                    h, qb, kt = items[i]
                    kv = h // 4
                    ptv, Bpt = state[i]
                    n = (h * NTB + qb) % 2
                    po = 4 + n
                    vcol = kt * 256 + kv * 128
                    P.op("pe", lambda e, po=po, vcol=vcol, ptv=ptv, kt=kt: e.matmul(bank(po), Vt[:, vcol:vcol + 128], ptv, start=(kt == 0), stop=(kt == 15)),
                         reads=[BV[kt // 2], Bpt], writes=[Bps[po]], same=False)
                    if kt % 2 == 1:
                        p0, Bp0 = state.pop(i - 1)
                        state.pop(i)
                        smv, Bsm = SMT[sm_n[0] % 8]
                        sm_n[0] += 1
                        P.op("dve", lambda e, smv=smv, p0=p0, ptv=ptv: e.tensor_tensor(smv, p0, ptv, ALU.add),
                             reads=[Bp0, Bpt], writes=[Bsm])
                        smstate[i] = (smv, Bsm)

                def sum_part(i):
                    h, qb, kt = items[i]
                    if kt % 2 == 0:
                        return
                    smv, Bsm = smstate.pop(i)
                    n = (h * NTB + qb) % 2
                    po, psm = 4 + n, 6 + n
                    P.op("pe", lambda e, psm=psm, smv=smv, kt=kt: e.matmul(bank(psm), ones, smv, start=(kt == 1), stop=(kt == 15)),
                         reads=[Bsm, Bmats], writes=[Bps[psm]], same=False)
                    if kt == 15:
                        si = n
                        for hh in range(2):
                            cs = slice(hh * 256, (hh + 1) * 256)
                            P.op("dve", lambda e, si=si, psm=psm, cs=cs: e.reciprocal(S[si][:, cs], ps[:, psm * 512 + cs.start:psm * 512 + cs.stop]),
                                 reads=[Bps[psm]], writes=[BS[si]], same=False)
                        P.op("dve", lambda e, si=si, po=po, h=h, qb=qb: e.tensor_tensor(yv(h, qb), bank(po), S[si][:], ALU.mult),
                             reads=[Bps[po], BS[si]], writes=[By[h][qb]])

                SK = 2
                LAG = 6
                NI = len(items)
                for i in range(NI + SK + LAG):
                    if i < NI:
                        s_part(i)
                    if SK <= i < NI + SK:
                        pv_part(i - SK)
                    if i >= SK + LAG:
                        sum_part(i - SK - LAG)
                for m in range(4):
                    Bc = BT8[2][m]
                    for F in (Fsm[2 * m], Fsm[2 * m + 1]):
                        toks = list(F.r.items())
                        if F.w is not None:
                            kw = ("e", F.w[1]) if F.w[0] == "eng" else ("d", F.w[1].num)
                            toks.append((kw, F.w))
                        for k_, v_ in toks:
                            if k_ not in Bc.r or Bc.r[k_][2] < v_[2]:
                                Bc.r[k_] = v_

                if stop_after >= 4:
                    out_proj(wbase_att, after_tb=lambda t: rmsnorm_tb(t, C_MLPG + 8))
                else:
                    out_proj(wbase_att)
            wbase += 5

            if stop_after >= 4:
                on = [0]

                def fin_out(c, t, si, s=s):
                    k = on[0] % 4
                    on[0] += 1
                    P.op("dve", lambda e, c=c, t=t, si=si, k=k: e.scalar_tensor_tensor(S[k][:], xv(c, t), col(C_FING + c), S[si][:], ALU.mult, ALU.mult),
                         reads=[Bx[c][t], BS[si], Bcst], writes=[BS[k]])
                    P.dma("sp", out_d[s, c * 128:(c + 1) * 128, t * TB:(t + 1) * TB], S[k][:], reads=[BS[k]])
                def fin_cb(t, s=s):
                    rmsnorm_tb(t, C_FING, out_fn=fin_out)
                    if s + 1 < nseq:
                        load_x(s + 1, t)
                        if t >= 1:
                            rmsnorm_tb(t - 1, C_MIXG + 0)
                mlp(wbase, C_MLPG + 8, do_norm=False, after_tb=fin_cb)
            wbase += 16
            if stop_after >= 4:
                pass
            else:
                for c in range(NCH):
                    P.dma("sp", out_d[s, c * 128:(c + 1) * 128, :], xt[:, c * L:(c + 1) * L], reads=Bx[c], owner=Bdbg)

        P.emit_all()
    return nc


def prep_weights(inp):
    f = np.float32
    w = np.zeros((NSLOT_PER_SEQ, 128, SLOT_ELEMS), dtype=f)
    i = 0
    w_in = np.asarray(inp["rg_w_in"], f)[0]
    w_a = np.asarray(inp["rg_w_a"], f)[0]
    w_x = np.asarray(inp["rg_w_x"], f)[0]
    win_r = w_in.reshape(8, 128, 2, 8, 128)
    for c in range(8):
        blk = win_r[:, :, :, c, :]
        w[i, :, 0:2048] = blk.transpose(1, 0, 2, 3).reshape(128, 2048)
        g = np.stack([w_a[0, c], w_x[0, c], w_a[1, c], w_x[1, c]], axis=1)
        w[i, :, 2048:2560] = g.reshape(128, 512)
        i += 1

    def proj_slots(W):
        nonlocal i
        Wr = W.reshape(8, 128, 2, 512)
        for s2 in range(2):
            w[i] = Wr[:, :, s2, :].transpose(1, 0, 2).reshape(128, 4096)
            i += 1

    def mlp_slots(Wu, Wd):
        nonlocal i
        Wur = Wu.reshape(8, 128, 8, 512)
        Wdr = Wd.reshape(8, 4, 128, 1024)
        for g in range(8):
            w[i] = Wur[:, :, g, :].transpose(1, 0, 2).reshape(128, 4096)
            i += 1
            w[i] = Wdr[g].transpose(1, 0, 2).reshape(128, 4096)
            i += 1

    proj_slots(np.asarray(inp["rg_w_out"], f)[0])
    mlp_slots(np.asarray(inp["mlp_w_up"], f)[0], np.asarray(inp["mlp_w_down"], f)[0])
    wqkv = np.asarray(inp["at_w_qkv"], f)[0]
    wr = wqkv.reshape(8, 128, 3, 512)
    for s3 in (2, 0, 1):
        w[i] = wr[:, :, s3, :].transpose(1, 0, 2).reshape(128, 4096)
        i += 1
    proj_slots(np.asarray(inp["at_w_o"], f)[0])
    mlp_slots(np.asarray(inp["mlp_w_up"], f)[1], np.asarray(inp["mlp_w_down"], f)[1])
    assert i == NSLOT_PER_SEQ
    return w


def prep_consts(inp):
    f = np.float32
    c = np.zeros((128, NCONST), dtype=f)

    def chunks(v):
        return np.asarray(v, f).reshape(8, 128).T

    for l in range(2):
        c[:, C_MIXG + 8 * l:C_MIXG + 8 * l + 8] = chunks(inp["norm_mix_g"][l])
        c[:, C_MLPG + 8 * l:C_MLPG + 8 * l + 8] = chunks(inp["norm_mlp_g"][l])
    c[:, C_FING:C_FING + 8] = chunks(inp["final_g"])
    cw = np.asarray(inp["rg_conv_w"], f)[0, :, 0, :]
    for j in range(4):
        c[:, C_CONVW + 8 * j:C_CONVW + 8 * j + 8] = chunks(cw[j])
    c[:, C_CONVB:C_CONVB + 8] = chunks(np.asarray(inp["rg_conv_b"], f)[0])
    for d in range(2):
        c[:, C_BA + 8 * d:C_BA + 8 * d + 8] = chunks(np.asarray(inp["rg_b_a"], f)[0, d])
        c[:, C_BX + 8 * d:C_BX + 8 * d + 8] = chunks(np.asarray(inp["rg_b_x"], f)[0, d])
        c[:, C_LAM + 8 * d:C_LAM + 8 * d + 8] = chunks(np.asarray(inp["rg_lam"], f)[0, d])
    c[:, C_QG] = np.asarray(inp["at_q_g"], f)[0]
    c[:, C_KG] = np.asarray(inp["at_k_g"], f)[0]
    return c


def prep_static():
    f = np.float32
    t = np.arange(L)
    row = (t // 64).astype(f)
    colp = (t % 64).astype(f)
    inv = (np.float32(10000.0) ** (-np.arange(32, dtype=f) / np.float32(32))).astype(f)
    rope = np.zeros((2, 128, L), dtype=f)
    for p in range(128):
        pos = row if p < 64 else colp
        ang = (pos * inv[p % 32]).astype(f)
        rope[0, p] = np.cos(ang)
        rope[1, p] = np.sin(ang)
    mats = np.zeros((128, 256), dtype=f)
    mats[:, 0:128] = 1.0
    for m in range(128):
        if m % 64 < 32:
            mats[m + 32, 128 + m] = -1.0
        else:
            mats[m - 32, 128 + m] = 1.0
    return rope, mats


_CACHE = {}


def kernel(**inputs):
    ncores = 8
    nseq = 2
    x = np.asarray(inputs["x"], np.float32)
    xT = np.ascontiguousarray(x.transpose(0, 2, 1))
    wst = prep_weights(inputs)
    cst = prep_consts(inputs)
    rope, mats = prep_static()
    if "nc" not in _CACHE:
        _CACHE["nc"] = build_program(nseq=nseq)
    nc = _CACHE["nc"]
    in_maps = []
    for i in range(ncores):
        in_maps.append({"xT": xT[i * nseq:(i + 1) * nseq], "wst": wst, "cst": cst, "rope": rope, "mats": mats})
    res = run_bass_kernel_spmd(nc, in_maps, core_ids=list(range(ncores)))
    outT = np.concatenate([np.asarray(r["outT"]) for r in res.results], axis=0)
    return np.ascontiguousarray(outT.transpose(0, 2, 1)).astype(np.float32)
```
